# Optimizing a Trainium2 kernel written in Bass

```python
import jax, jax.numpy as jnp
from jax import lax
import numpy as np

D_MODEL = 1024
BATCH = 2
SEQ = 8192
DEPTH = 2

CHUNK = 64
MIX_WIDTH = D_MODEL
CONV_WIDTH_CH = MIX_WIDTH // 2
CONV_HEADS = 8
CONV_K = 3
POOL_WIDTH_CH = MIX_WIDTH - CONV_WIDTH_CH
POOL_WINDOWS = (2, 4, 8, 16)
N_POOL_GROUPS = len(POOL_WINDOWS)
POOL_GC = POOL_WIDTH_CH // N_POOL_GROUPS
IN_COLS = 4 * CONV_WIDTH_CH + 2 * POOL_WIDTH_CH
RMS_EPS = 1e-6

kernel_name = "hybrid_shortconv_pool_sandwich"


def rmsnorm(x, g):
    xf = x.astype(jnp.float32)
    inv = lax.rsqrt(jnp.mean(xf * xf, axis=-1, keepdims=True) + RMS_EPS)
    return (xf * inv).astype(x.dtype) * g


def causal_depthwise_conv3(v, w, b):
    S = v.shape[1]
    vp = jnp.pad(v, ((0, 0), (CONV_K - 1, 0), (0, 0)))
    y = w[0] * vp[:, 0:S] + w[1] * vp[:, 1:S + 1] + w[2] * vp[:, 2:S + 2]
    return y + b


def multiscale_pool(u, w_pool, scale):
    Bz, S, C = u.shape
    uf = u.astype(jnp.float32)
    cs = jnp.cumsum(uf, axis=1)
    count_pos = jnp.arange(1, S + 1, dtype=jnp.float32)[None, :, None]
    outs = []
    for g, w in enumerate(POOL_WINDOWS):
        sl = slice(g * POOL_GC, (g + 1) * POOL_GC)
        csg = cs[..., sl]
        prev = jnp.pad(csg, ((0, 0), (w, 0), (0, 0)))[:, :S]
        mean = (csg - prev) / jnp.minimum(count_pos, float(w))
        outs.append(mean - uf[..., sl])
    p = jnp.concatenate(outs, axis=-1).astype(u.dtype).reshape(Bz, S, N_POOL_GROUPS, POOL_GC)
    y = jnp.einsum('bsgc,gcd->bsgd', p, w_pool).reshape(Bz, S, C)
    return y * scale


def setup_inputs(seed: int = 0) -> dict:
    key = jax.random.key(seed)
    ks = jax.random.split(key, 10)
    f32 = jnp.float32
    x = jax.random.normal(ks[0], (BATCH, SEQ, D_MODEL), f32)
    pre_norm = 1.0 + 0.05 * jax.random.normal(ks[1], (DEPTH, D_MODEL), f32)
    w_in = jax.random.normal(ks[2], (DEPTH, D_MODEL, IN_COLS), f32) * D_MODEL ** -0.5
    conv_w = jax.random.normal(ks[3], (DEPTH, CONV_K, CONV_WIDTH_CH), f32) * CONV_K ** -0.5
    conv_b = 0.02 * jax.random.normal(ks[4], (DEPTH, CONV_WIDTH_CH), f32)
    w_pool = jax.random.normal(ks[5], (DEPTH, N_POOL_GROUPS, POOL_GC, POOL_GC), f32) * POOL_GC ** -0.5
    pool_scale = 1.0 + 0.05 * jax.random.normal(ks[6], (DEPTH, POOL_WIDTH_CH), f32)
    w_out = jax.random.normal(ks[7], (DEPTH, MIX_WIDTH, D_MODEL), f32) * MIX_WIDTH ** -0.5
    post_norm = 1.0 + 0.05 * jax.random.normal(ks[8], (DEPTH, D_MODEL), f32)
    return {"x": x, "pre_norm": pre_norm, "w_in": w_in, "conv_w": conv_w, "conv_b": conv_b,
            "w_pool": w_pool, "pool_scale": pool_scale, "w_out": w_out, "post_norm": post_norm}


def reference(x, pre_norm, w_in, conv_w, conv_b, w_pool, pool_scale, w_out, post_norm):
    A = CONV_WIDTH_CH
    P = POOL_WIDTH_CH
    for l in range(DEPTH):
        hn = rmsnorm(x, pre_norm[l])
        proj = jnp.einsum('bsd,dc->bsc', hn, w_in[l])
        b_a = proj[..., 0:A]
        c_a = proj[..., A:2 * A]
        h_a = proj[..., 2 * A:3 * A]
        z_a = proj[..., 3 * A:4 * A]
        u_b = proj[..., 4 * A:4 * A + P]
        z_b = proj[..., 4 * A + P:4 * A + 2 * P]
        y_a = b_a * causal_depthwise_conv3(c_a * h_a, conv_w[l], conv_b[l])
        y_a = y_a * jax.nn.silu(z_a)
        y_b = multiscale_pool(u_b, w_pool[l], pool_scale[l]) * jax.nn.silu(z_b)
        y = jnp.concatenate([y_a, y_b], axis=-1)
        out = jnp.einsum('bsc,cd->bsd', y, w_out[l])
        x = x + rmsnorm(out, post_norm[l])
    return x
```

```python
import contextlib
import numpy as np
import concourse.bass as bass
import concourse.mybir as mybir
from concourse.bass_utils import run_bass_kernel_spmd

F32 = mybir.dt.float32
BF16 = mybir.dt.bfloat16
ALU = mybir.AluOpType
AF = mybir.ActivationFunctionType

D = 1024
BATCH = 2
SEQ = 8192
DEPTH = 2
NCORES = 8
TOK = 2048
HALO = 32
T = TOK + HALO
W = 416
NT = T // W
KC = D // 128
GW = 16 + W
EPS = 1e-6
POOL_WINDOWS = (2, 4, 8, 16)

PC_PRE, PC_POST, PC_CW, PC_CB, PC_PS = 0, 8, 16, 28, 32
PL = 36
PC_TAB = DEPTH * PL
PC_EPS = PC_TAB + 64
NPAR = PC_EPS + 1

FUSED = True


class Res:
    __slots__ = ("name", "last_w", "readers", "excl")

    def __init__(self, name, excl=False):
        self.name = name
        self.last_w = None
        self.readers = []
        self.excl = excl


class DmaSem:
    __slots__ = ("handle", "count")

    def __init__(self, handle):
        self.handle = handle
        self.count = 0


class Op:
    __slots__ = ("eng", "fn", "deps", "needs_inc", "dsem", "dval", "ticket")


ENGS = ("pe", "act", "dve", "pool", "sp")


class Prog:
    def __init__(self):
        self.q = {e: [] for e in ENGS}

    def add(self, eng, fn, reads=(), writes=(), dsem=None):
        op = Op()
        op.eng, op.fn, op.dsem, op.needs_inc, op.ticket, op.dval = eng, fn, dsem, False, None, None
        deps = []
        seen = set()

        def dep(o):
            if o is None or id(o) in seen:
                return
            seen.add(id(o))
            deps.append(o)

        for r in reads:
            dep(r.last_w)
            if r.excl:
                for o in r.readers:
                    if o.eng != eng:
                        dep(o)
        for w in writes:
            dep(w.last_w)
            for o in w.readers:
                dep(o)
        for r in reads:
            r.readers.append(op)
        for w in writes:
            w.last_w = op
            w.readers = []
        op.deps = [o for o in deps if not (eng == "pe" and o.eng == "pe" and o.dsem is None)]
        for o in op.deps:
            if o.dsem is None:
                o.needs_inc = True
        if dsem is not None:
            dsem.count += 1
            op.dval = 16 * dsem.count
        self.q[eng].append(op)
        return op

    def emit(self, eng, engine, esem):
        waited = {}
        for op in self.q[eng]:
            for o in op.deps:
                if o.dsem is not None:
                    key, h, val = id(o.dsem), o.dsem.handle, o.dval
                else:
                    key, h, val = o.eng, esem[o.eng], o.ticket
                if waited.get(key, 0) >= val:
                    continue
                engine.wait_ge(h, val)
                waited[key] = val
            if op.fn is None:
                continue
            inst = op.fn(engine)
            if op.dsem is not None:
                inst.then_inc(op.dsem.handle, 16)
            elif op.needs_inc:
                inst.then_inc(esem[eng], 1)

    def assign_tickets(self):
        for e in ENGS:
            n = 0
            for op in self.q[e]:
                if op.dsem is None and op.needs_inc:
                    n += 1
                    op.ticket = n


def build_nc(layers):
    nc = bass.Bass("TRN2", target_bir_lowering=False, dynamic_dma_scratch_size=8192)
    xT = nc.dram_tensor("xT", [D, T], F32, kind="ExternalInput").ap()
    w_in = nc.dram_tensor("w_in", [DEPTH, D, 3 * D], F32, kind="ExternalInput").ap()
    w_out = nc.dram_tensor("w_out", [DEPTH, D, D], F32, kind="ExternalInput").ap()
    w_pool = nc.dram_tensor("w_pool", [DEPTH, 4, 128, 128], F32, kind="ExternalInput").ap()
    par = nc.dram_tensor("par", [128, NPAR], F32, kind="ExternalInput").ap()
    oT = nc.dram_tensor("oT", [D, T], F32, kind="ExternalOutput").ap()

    xT_v = xT.rearrange("(k p) t -> p k t", p=128)
    oT_v = oT.rearrange("(k p) t -> p k t", p=128)

    P = Prog()
    with contextlib.ExitStack() as es:
        def sb(name, shape, dt):
            return es.enter_context(nc.sbuf_tensor(name, shape, dt))

        def sem(name):
            return es.enter_context(nc.semaphore(name))

        x_sb = sb("x_sb", [128, KC, T], F32)
        xs_sb = sb("xs_sb", [128, KC, T], BF16)
        y_sb = sb("y_sb", [128, KC, T], BF16)
        win_sb = [sb(f"win{s}", [128, KC, 512], BF16) for s in range(2)]
        wout_sb = sb("wout", [128, KC, D], BF16)
        wpool_sb = sb("wpool", [128, DEPTH, 4, 128], BF16)
        par_sb = sb("par_sb", [128, NPAR], F32)
        ones_sb = sb("ones", [128, 128], BF16)
        corr_sb = sb("corr", [128, 16], F32)
        utail = sb("utail", [128, 4, 16], F32)
        NG = 14
        G = [sb(f"g{i}", [128, GW], F32) for i in range(NG)]
        OB = [sb(f"ob{i}", [128, W], F32) for i in range(KC)]
        BL = [sb(f"bl{i}", [128, W], F32) for i in range(2)]
        BT = [sb(f"bt{i}", [128, W], F32) for i in range(2)]
        pbuf = [sb(f"pb{i}", [128, W], BF16) for i in range(2)]
        sqb = [sb(f"sq{i}", [128, W], BF16) for i in range(3)]
        sqp = [sb(f"sqp{i}", [128, W], BF16) for i in range(4)]
        banks = [es.enter_context(nc.psum_tensor(f"bank{i}", [128, 512], F32)) for i in range(8)]

        esem = {e: sem(f"e_{e}") for e in ENGS}
        ds_x = [DmaSem(sem(f"d_x{i}")) for i in range(NT)]
        ds_win = [DmaSem(sem(f"d_win{i}")) for i in range(2)]
        ds_wout = DmaSem(sem("d_wout"))
        ds_wp = [DmaSem(sem(f"d_wp{l}")) for l in range(DEPTH)]
        ds_par = DmaSem(sem("d_par"))
        ds_out = DmaSem(sem("d_out"))

        r_x = [[Res(f"x{i}_{k}") for k in range(KC)] for i in range(NT)]
        r_xs = [[Res(f"xs{i}_{k}") for k in range(KC)] for i in range(NT)]
        r_y = [[Res(f"y{i}_{k}") for k in range(KC)] for i in range(NT)]
        r_win = [Res(f"win{s}") for s in range(2)]
        r_wout = Res("wout")
        r_wpool = [Res(f"wpool{l}") for l in range(DEPTH)]
        r_par = Res("par")
        r_ones = Res("ones")
        r_corr = Res("corr")
        r_utail = [Res(f"utail{g}") for g in range(4)]
        r_G = [Res(f"g{i}") for i in range(NG)]
        r_Gh = [Res(f"gh{i}") for i in range(NG)]
        r_OB = [Res(f"ob{i}") for i in range(KC)]
        r_BL = [Res(f"bl{i}") for i in range(2)]
        r_BT = [Res(f"bt{i}") for i in range(2)]
        r_pb = [Res(f"pb{i}") for i in range(2)]
        r_sq = [Res(f"sq{i}") for i in range(3)]
        r_sqp = [Res(f"sqp{i}") for i in range(4)]
        r_bank = [Res(f"bank{i}", excl=True) for i in range(8)]

        state = {"bank": 0, "sq": 0, "g": {}, "pb": 0}

        def next_bank():
            b = state["bank"]
            state["bank"] = (b + 1) % 6
            return b

        def next_aux():
            b = state.get("aux", 0)
            state["aux"] = 1 - b
            return 6 + b

        def next_sq():
            b = state["sq"]
            state["sq"] = (b + 1) % 3
            return b

        def rot(name, choices):
            i = state["g"].get(name, 0)
            state["g"][name] = i + 1
            return choices[i % len(choices)]

        def tcols(i):
            return slice(i * W, (i + 1) * W)

        def pc(c):
            return par_sb[:, c:c + 1]

        P.add("sp", lambda e: e.dma_start(out=par_sb[:], in_=par[:, :]), writes=[r_par], dsem=ds_par)
        P.add("pool", lambda e: e.memset(ones_sb[:], 1.0), writes=[r_ones])
        ds_x0 = [DmaSem(sem(f"d_x0_{k}")) for k in range(KC)]
        for k in range(KC):
            P.add("sp", lambda e, k=k: e.dma_start(out=x_sb[:, k, tcols(0)], in_=xT_v[:, k, tcols(0)]),
                  writes=[r_x[0][k]], dsem=ds_x0[k])
        for i in range(1, NT):
            P.add("sp", lambda e, i=i: e.dma_start(out=x_sb[:, :, tcols(i)], in_=xT_v[:, :, tcols(i)]),
                  writes=r_x[i], dsem=ds_x[i])
        for l in layers:
            P.add("pool", lambda e, l=l: e.dma_start(out=wpool_sb[:, l, :, :],
                                                     in_=w_pool[l].rearrange("g c d -> c g d")),
                  writes=[r_wpool[l]], dsem=ds_wp[l])

        wstate = {"n": 0}

        def load_win(l, sc):
            slot = wstate["n"] % 2
            wstate["n"] += 1
            wv = w_in[l].rearrange("(k p) c -> p k c", p=128)
            off, ncol = (sc * 512, 512) if sc < 4 else (2048 + (sc - 4) * 256, 256)
            P.add("pool", lambda e: e.dma_start(out=win_sb[slot][:, :, 0:ncol], in_=wv[:, :, off:off + ncol]),
                  writes=[r_win[slot]], dsem=ds_win[slot])
            return slot

        def load_wout(l):
            wv = w_out[l].rearrange("(k p) c -> p k c", p=128)
            P.add("pool", lambda e: e.dma_start(out=wout_sb[:], in_=wv), writes=[r_wout], dsem=ds_wout)

        def rstd_from_bank(b, g_ln, g_ri):
            P.add("act", lambda e: e.activation(out=G[g_ln][:, 0:W], in_=banks[b][:, 0:W], func=AF.Ln,
                                                scale=1.0 / D, bias=pc(PC_EPS)),
                  reads=[r_bank[b], r_par], writes=[r_G[g_ln]])
            P.add("act", lambda e: e.activation(out=G[g_ri][:, 0:W], in_=G[g_ln][:, 0:W], func=AF.Exp, scale=-0.5),
                  reads=[r_G[g_ln]], writes=[r_G[g_ri]])
            return g_ri

        def prenorm_squares(i):
            b = 7
            for k in range(KC):
                s = next_sq()
                if k in (2, 4, 6):
                    P.add("pool", lambda e, s=s, k=k: e.tensor_tensor(out=sqb[s][:], in0=x_sb[:, k, tcols(i)],
                                                                      in1=x_sb[:, k, tcols(i)], op=ALU.mult),
                          reads=[r_x[i][k]], writes=[r_sq[s]])
                else:
                    P.add("act", lambda e, s=s, k=k: e.activation(out=sqb[s][:], in_=x_sb[:, k, tcols(i)],
                                                                  func=AF.Square),
                          reads=[r_x[i][k]], writes=[r_sq[s]])
                P.add("pe", lambda e, s=s, k=k: e.matmul(banks[b][:, 0:W], lhsT=ones_sb[:], rhs=sqb[s][:],
                                                         start=(k == 0), stop=(k == KC - 1)),
                      reads=[r_sq[s], r_ones], writes=[r_bank[b]])
            return b

        pn_buf = {}

        def prenorm_finish(l, i, b, part=None):
            if part in (None, 0):
                pn_buf[(l, i)] = rot("pn", [12, 13])
                rstd_from_bank(b, pn_buf[(l, i)], pn_buf[(l, i)])
            g_ri = pn_buf[(l, i)]
            for k in (range(KC) if part is None else range(4 * part, 4 * part + 4)):
                P.add("dve", lambda e, k=k: e.scalar_tensor_tensor(
                    out=xs_sb[:, k, tcols(i)], in0=x_sb[:, k, tcols(i)], scalar=pc(l * PL + PC_PRE + k),
                    op0=ALU.mult, in1=G[g_ri][:, 0:W], op1=ALU.mult),
                    reads=[r_x[i][k], r_G[g_ri], r_par], writes=[r_xs[i][k]])

        def flush(pe_list):
            for fn, rd, wr in pe_list:
                P.add("pe", fn, reads=rd, writes=wr)
            pe_list.clear()

        def prenorm_tile(l, i):
            b = prenorm_squares(i)
            prenorm_finish(l, i, b)

        def prenorm_gen(l, i):
            b = 7

            def squares(k0):
                for k in range(k0, k0 + 4):
                    if k % 4 == 1:
                        P.add("pool", lambda e, k=k: e.tensor_tensor(out=sqp[k % 4][:], in0=x_sb[:, k, tcols(i)],
                                                                     in1=x_sb[:, k, tcols(i)], op=ALU.mult),
                              reads=[r_x[i][k]], writes=[r_sqp[k % 4]])
                    else:
                        P.add("act", lambda e, k=k: e.activation(out=sqp[k % 4][:], in_=x_sb[:, k, tcols(i)],
                                                                 func=AF.Square),
                              reads=[r_x[i][k]], writes=[r_sqp[k % 4]])

            def ones(k0):
                for k in range(k0, k0 + 4):
                    P.add("pe", lambda e, k=k: e.matmul(banks[b][:, 0:W], lhsT=ones_sb[:], rhs=sqp[k % 4][:],
                                                        start=(k == 0), stop=(k == KC - 1)),
                          reads=[r_sqp[k % 4], r_ones], writes=[r_bank[b]])
            squares(0)
            yield
            yield
            ones(0)
            squares(4)
            yield
            ones(4)
            yield
            prenorm_finish(l, i, b, 0)
            yield
            prenorm_finish(l, i, b, 1)
            yield

        def proj_group(slot, q, i):
            b = next_bank()

            def fn(e):
                ins = None
                for k in range(KC):
                    ins = e.matmul(banks[b][:, 0:W], lhsT=win_sb[slot][:, k, q * 128:(q + 1) * 128],
                                   rhs=xs_sb[:, k, tcols(i)], start=(k == 0), stop=(k == KC - 1))
                return ins
            P.add("pe", fn, reads=r_xs[i] + [r_win[slot]], writes=[r_bank[b]])
            return b

        def step_A(l, j, i, slot, prev, ret):
            base = l * PL
            g_c = rot("c", [0, 1])
            g_ch = rot("ch", [2, 3])
            g_acc = rot("acc", [4, 5])
            g_sz = rot("sz", [6, 7])
            ret[0] = g_ch
            bc = proj_group(slot, 1, i)
            P.add("act", lambda e: e.activation(out=G[g_c][:, 0:W], in_=banks[bc][:, 0:W], func=AF.Copy),
                  reads=[r_bank[bc]], writes=[r_G[g_c]])
            yield
            bh = proj_group(slot, 2, i)
            P.add("dve", lambda e: e.tensor_tensor(out=G[g_ch][:, 16:GW], in0=G[g_c][:, 0:W], in1=banks[bh][:, 0:W],
                                                   op=ALU.mult),
                  reads=[r_G[g_c], r_bank[bh]], writes=[r_G[g_ch]])
            if prev is None:
                P.add("pool", lambda e: e.memset(G[g_ch][:, 0:16], 0.0), writes=[r_Gh[g_ch]])
            else:
                P.add("pool", lambda e: e.tensor_copy(out=G[g_ch][:, 0:16], in_=G[prev][:, W:GW]),
                      reads=[r_G[prev]], writes=[r_Gh[g_ch]])
            yield
            bz = proj_group(slot, 3, i)
            P.add("act", lambda e: e.activation(out=G[g_sz][:, 0:W], in_=banks[bz][:, 0:W], func=AF.Silu),
                  reads=[r_bank[bz]], writes=[r_G[g_sz]])
            P.add("pool", lambda e: e.tensor_scalar(out=G[g_acc][:, 0:W], in0=G[g_ch][:, 16:GW],
                                                    scalar1=pc(base + PC_CW + 2 * 4 + j), scalar2=pc(base + PC_CB + j),
                                                    op0=ALU.mult, op1=ALU.add),
                  reads=[r_G[g_ch], r_par], writes=[r_G[g_acc]])
            P.add("dve", lambda e: e.scalar_tensor_tensor(
                out=G[g_acc][:, 0:W], in0=G[g_ch][:, 15:GW - 1], scalar=pc(base + PC_CW + 1 * 4 + j), op0=ALU.mult,
                in1=G[g_acc][:, 0:W], op1=ALU.add),
                reads=[r_G[g_ch], r_Gh[g_ch], r_G[g_acc], r_par], writes=[r_G[g_acc]])
            P.add("dve", lambda e: e.scalar_tensor_tensor(
                out=G[g_acc][:, 0:W], in0=G[g_ch][:, 14:GW - 2], scalar=pc(base + PC_CW + 0 * 4 + j), op0=ALU.mult,
                in1=G[g_acc][:, 0:W], op1=ALU.add),
                reads=[r_G[g_ch], r_Gh[g_ch], r_G[g_acc], r_par], writes=[r_G[g_acc]])
            yield
            bb = proj_group(slot, 0, i)
            P.add("dve", lambda e: e.tensor_tensor(out=G[g_acc][:, 0:W], in0=G[g_acc][:, 0:W], in1=banks[bb][:, 0:W],
                                                   op=ALU.mult),
                  reads=[r_G[g_acc], r_bank[bb]], writes=[r_G[g_acc]])
            P.add("pool", lambda e: e.tensor_tensor(out=y_sb[:, j, tcols(i)], in0=G[g_acc][:, 0:W], in1=G[g_sz][:, 0:W],
                                                    op=ALU.mult),
                  reads=[r_G[g_acc], r_G[g_sz]], writes=[r_y[i][j]])
            yield

        def step_B(l, g, i, slot, prev, deferred):
            base = l * PL
            w = POOL_WINDOWS[g]
            bu = proj_group(slot, 0, i)
            bz = proj_group(slot, 1, i)
            g_u = rot("u", [8, 9])
            g_sz = rot("sz", [6, 7])
            sA, sB = 10, 11
            pb = state["pb"]
            state["pb"] = 1 - pb
            P.add("act", lambda e: e.activation(out=G[g_u][:, 16:GW], in_=banks[bu][:, 0:W], func=AF.Copy),
                  reads=[r_bank[bu]], writes=[r_G[g_u]])
            if i == 0:
                P.add("pool", lambda e: e.memset(G[g_u][:, 0:16], 0.0), writes=[r_Gh[g_u]])
            else:
                P.add("pool", lambda e: e.tensor_copy(out=G[g_u][:, 0:16], in_=utail[:, g, :]),
                      reads=[r_utail[g]], writes=[r_Gh[g_u]])
            if i < NT - 1:
                P.add("pool", lambda e: e.tensor_copy(out=utail[:, g, :], in_=G[g_u][:, W:GW]),
                      reads=[r_G[g_u]], writes=[r_utail[g]])
            P.add("act", lambda e: e.activation(out=G[g_sz][:, 0:W], in_=banks[bz][:, 0:W], func=AF.Silu),
                  reads=[r_bank[bz]], writes=[r_G[g_sz]])
            src, dst = g_u, sA
            v = 2
            while v <= w:
                st = 16 - (w - v)
                h = v // 2
                rd = [r_G[src], r_Gh[src]]
                wr = [r_G[dst], r_Gh[dst]]
                P.add("dve", lambda e, src=src, dst=dst, st=st, h=h: e.tensor_tensor(
                    out=G[dst][:, st:GW], in0=G[src][:, st:GW], in1=G[src][:, st - h:GW - h], op=ALU.add),
                    reads=rd, writes=wr)
                src = dst
                dst = sB if dst == sA else sA
                v *= 2
            s_fin = src
            P.add("dve", lambda e: e.scalar_tensor_tensor(
                out=pbuf[pb][:, 0:W], in0=G[s_fin][:, 16:GW], scalar=1.0 / w, op0=ALU.mult,
                in1=G[g_u][:, 16:GW], op1=ALU.subtract),
                reads=[r_G[s_fin], r_G[g_u]], writes=[r_pb[pb]])
            if i == 0:
                c0 = HALO
                P.add("dve", lambda e: e.tensor_tensor(out=corr_sb[:, 0:16], in0=G[s_fin][:, 16 + c0:32 + c0],
                                                       in1=par_sb[:, PC_TAB + 16 * g:PC_TAB + 16 * g + 16], op=ALU.mult),
                      reads=[r_G[s_fin], r_par], writes=[r_corr])
                P.add("dve", lambda e: e.tensor_tensor(out=pbuf[pb][:, c0:c0 + 16], in0=corr_sb[:, 0:16],
                                                       in1=G[g_u][:, 16 + c0:32 + c0], op=ALU.subtract),
                      reads=[r_corr, r_G[g_u], r_pb[pb]], writes=[r_pb[pb]])

            def later():
                bp = next_bank()
                P.add("pe", lambda e: e.matmul(banks[bp][:, 0:W], lhsT=wpool_sb[:, l, g, :], rhs=pbuf[pb][:, 0:W],
                                               start=True, stop=True),
                      reads=[r_pb[pb], r_wpool[l]], writes=[r_bank[bp]])
                P.add("dve", lambda e: e.scalar_tensor_tensor(
                    out=y_sb[:, 4 + g, tcols(i)], in0=banks[bp][:, 0:W], scalar=pc(base + PC_PS + g), op0=ALU.mult,
                    in1=G[g_sz][:, 0:W], op1=ALU.mult),
                    reads=[r_bank[bp], r_G[g_sz], r_par], writes=[r_y[i][4 + g]])
            deferred.append(later)
            return g_u

        def phase_B_tile(l, i, last):
            base = l * PL
            bs = 6
            pend = []
            if last and i == NT - 1:
                P.add("act", lambda e: e.activation(out=corr_sb[:, 0:2], in_=corr_sb[:, 2:4], func=AF.Exp),
                      reads=[r_corr], writes=[r_corr])
            for m in range(KC):
                b = next_bank()

                def fn(e, b=b, m=m):
                    ins = None
                    for k in range(KC):
                        ins = e.matmul(banks[b][:, 0:W], lhsT=wout_sb[:, k, m * 128:(m + 1) * 128],
                                       rhs=y_sb[:, k, tcols(i)], start=(k == 0), stop=(k == KC - 1))
                    return ins
                P.add("pe", fn, reads=r_y[i] + [r_wout], writes=[r_bank[b]])
                flush(pend)
                s = next_sq()
                P.add("act", lambda e, b=b, m=m: e.activation(out=OB[m][:], in_=banks[b][:, 0:W], func=AF.Copy),
                      reads=[r_bank[b]], writes=[r_OB[m]])
                P.add("act", lambda e, b=b, s=s: e.activation(out=sqb[s][:], in_=banks[b][:, 0:W], func=AF.Square),
                      reads=[r_bank[b]], writes=[r_sq[s]])
                pend.append((lambda e, s=s, m=m: e.matmul(banks[bs][:, 0:W], lhsT=ones_sb[:], rhs=sqb[s][:],
                                                          start=(m == 0), stop=(m == KC - 1)),
                             [r_sq[s], r_ones], [r_bank[bs]]))
                if m in (2, 5):
                    yield
            yield
            flush(pend)
            yield
            bl = rot("bl", [0, 1])
            P.add("act", lambda e: e.activation(out=BL[bl][:], in_=banks[bs][:, 0:W], func=AF.Ln,
                                                scale=1.0 / D, bias=pc(PC_EPS)),
                  reads=[r_bank[bs], r_par], writes=[r_BL[bl]])
            P.add("act", lambda e: e.activation(out=BL[bl][:], in_=BL[bl][:], func=AF.Exp, scale=-0.5),
                  reads=[r_BL[bl]], writes=[r_BL[bl]])
            for m in range(KC):
                t = m % 2
                P.add("dve", lambda e, m=m, t=t: e.scalar_tensor_tensor(
                    out=BT[t][:], in0=OB[m][:], scalar=pc(base + PC_POST + m), op0=ALU.mult,
                    in1=BL[bl][:], op1=ALU.mult),
                    reads=[r_OB[m], r_BL[bl], r_par], writes=[r_BT[t]])
                add_eng = "pool" if (last and i == NT - 1 and m % 2 == 0) else "dve"
                P.add(add_eng, lambda e, m=m, t=t: e.tensor_tensor(
                    out=x_sb[:, m, tcols(i)], in0=x_sb[:, m, tcols(i)], in1=BT[t][:], op=ALU.add),
                    reads=[r_x[i][m], r_BT[t]], writes=[r_x[i][m]])
                if last:
                    P.add("sp", lambda e, m=m: e.dma_start(out=oT_v[:, m, tcols(i)], in_=x_sb[:, m, tcols(i)]),
                          reads=[r_x[i][m]], dsem=ds_out)
                if m in (2, 5, 7):
                    yield

        SC_ORDER = [4, 5, 6, 7, 0, 1, 2, 3]
        nL = len(layers)
        blocks = [(li, pos) for li in range(nL) for pos in range(8)]

        def interleave(gens, cycles=None):
            gens = list(gens)
            n = 0
            while gens and (cycles is None or n < cycles):
                for gen in gens[:]:
                    try:
                        next(gen)
                    except StopIteration:
                        gens.remove(gen)
                n += 1
            return gens

        def one_shot(fn):
            fn()
            yield

        pn_at = {}
        for li in range(1, nL):
            for t in range(NT):
                key = (li - 1, 7, t + 3) if t + 3 < NT else (li, 0, t + 3 - NT)
                pn_at.setdefault(key, []).append((li, t))

        order = []
        for bi, (li, pos) in enumerate(blocks):
            if li > 0 and pos == 0:
                order += [(bi, 0), (bi, 1), (bi, 2), (bi + 1, 0), (bi + 1, 1), (bi + 1, 2),
                          (bi, 3), (bi, 4), (bi + 1, 3), (bi + 1, 4)]
            elif li > 0 and pos == 1:
                continue
            else:
                order += [(bi, i) for i in range(NT)]

        def load_blk(bi):
            bli, bpos = blocks[bi]
            return load_win(layers[bli], SC_ORDER[bpos])

        slots = {0: load_blk(0)}
        if len(blocks) > 1:
            slots[1] = load_blk(1)
        deferred = []
        carry = []
        prevs = {}
        done = {}
        for (bi, i) in order:
            li, pos = blocks[bi]
            l = layers[li]
            sc = SC_ORDER[pos]
            if pos == 1 and i == 0:
                load_wout(l)
            slot = slots[bi]
            prev = prevs.get(bi)
            if li == 0 and pos == 0:
                if i == 0:
                    prenorm_tile(l, 0)
                    for t in range(1, NT):
                        prenorm_tile(l, t)
            pn_gens = [prenorm_gen(layers[pli], t) for (pli, t) in pn_at.get((li, pos, i), [])]
            run = deferred[:]
            deferred.clear()
            gens = []
            ret = [None]
            if sc < 4:
                gens.append(step_A(l, sc, i, slot, prev, ret))
            else:
                def do_B(sc=sc, i=i, slot=slot, prev=prev, ret=ret, l=l):
                    ret[0] = step_B(l, sc - 4, i, slot, prev, deferred)
                gens.append(one_shot(do_B))
            if pos == 7 and i >= 1:
                gens.append(phase_B_tile(l, i - 1, li == nL - 1))
            if pos == 0 and li > 0 and i == 0:
                gens.append(phase_B_tile(layers[li - 1], NT - 1, False))
            carry = interleave(carry + gens + pn_gens, cycles=4)
            prevs[bi] = ret[0]
            for fn in run:
                fn()
            done[bi] = done.get(bi, 0) + 1
            if done[bi] == NT and bi + 2 < len(blocks):
                slots[bi + 2] = load_blk(bi + 2)
        for fn in deferred:
            fn()
        deferred.clear()
        interleave(carry + [phase_B_tile(layers[nL - 1], NT - 1, True)])

        P.add("sp", None, reads=[], writes=[])
        P.q["sp"][-1].deps = [o for o in P.q["sp"] if o.dsem is ds_out][-1:]

        P.assign_tickets()
        with nc.Block() as block:
            @block.tensor
            def _(e):
                P.emit("pe", e, esem)

            @block.scalar
            def _(e):
                P.emit("act", e, esem)

            @block.vector
            def _(e):
                P.emit("dve", e, esem)

            @block.gpsimd
            def _(e):
                P.emit("pool", e, esem)

            @block.sync
            def _(e):
                P.emit("sp", e, esem)
    return nc


def _pack_params(pre_norm, conv_w, conv_b, pool_scale, post_norm, core):
    par = np.zeros((128, NPAR), np.float32)
    for l in range(DEPTH):
        b = l * PL
        par[:, b + PC_PRE:b + PC_PRE + 8] = pre_norm[l].reshape(8, 128).T
        par[:, b + PC_POST:b + PC_POST + 8] = post_norm[l].reshape(8, 128).T
        for tap in range(3):
            par[:, b + PC_CW + tap * 4:b + PC_CW + tap * 4 + 4] = conv_w[l, tap].reshape(4, 128).T
        par[:, b + PC_CB:b + PC_CB + 4] = conv_b[l].reshape(4, 128).T
        par[:, b + PC_PS:b + PC_PS + 4] = pool_scale[l].reshape(4, 128).T
    seq_start = (core % (NCORES // BATCH) == 0)
    for g, w in enumerate(POOL_WINDOWS):
        for j in range(16):
            cnt = min(j + 1, w) if seq_start else w
            par[:, PC_TAB + 16 * g + j] = np.float32(1.0) / np.float32(cnt)
    par[:, PC_EPS] = EPS
    return par


def _prep_w_in(w_in):
    w_in = np.asarray(w_in, np.float32)
    cols = []
    for j in range(4):
        for q in range(4):
            cols.append(np.arange(j * 128 + q * 512, j * 128 + q * 512 + 128))
    for g in range(4):
        cols.append(np.arange(2048 + g * 128, 2048 + g * 128 + 128))
        cols.append(np.arange(2560 + g * 128, 2560 + g * 128 + 128))
    return np.ascontiguousarray(w_in[:, :, np.concatenate(cols)])


_NC_CACHE = {}


def _get_nc(layers):
    key = tuple(layers)
    if key not in _NC_CACHE:
        _NC_CACHE[key] = build_nc(list(layers))
    return _NC_CACHE[key]


def _shard_x(x):
    per = NCORES // BATCH
    out = []
    for c in range(NCORES):
        b, s0 = c // per, (c % per) * TOK
        reg = np.zeros((T, D), np.float32)
        lo = s0 - HALO
        if lo < 0:
            reg[HALO:] = x[b, 0:TOK]
        else:
            reg[:] = x[b, lo:s0 + TOK]
        out.append(np.ascontiguousarray(reg.T))
    return out


def kernel(x, pre_norm, w_in, conv_w, conv_b, w_pool, pool_scale, w_out, post_norm):
    x = np.asarray(x, np.float32)
    w_in = _prep_w_in(w_in)
    w_out = np.ascontiguousarray(np.asarray(w_out, np.float32))
    w_pool = np.ascontiguousarray(np.asarray(w_pool, np.float32))
    pre_norm, post_norm = np.asarray(pre_norm, np.float32), np.asarray(post_norm, np.float32)
    conv_w, conv_b = np.asarray(conv_w, np.float32), np.asarray(conv_b, np.float32)
    pool_scale = np.asarray(pool_scale, np.float32)
    pars = [_pack_params(pre_norm, conv_w, conv_b, pool_scale, post_norm, c) for c in range(NCORES)]
    xs = _shard_x(x)
    launches = [list(range(DEPTH))] if FUSED else [[l] for l in range(DEPTH)]
    for layers in launches:
        nc = _get_nc(layers)
        in_maps = [{"xT": xs[c], "w_in": w_in, "w_out": w_out, "w_pool": w_pool, "par": pars[c]}
                   for c in range(NCORES)]
        res = run_bass_kernel_spmd(nc, in_maps, core_ids=list(range(NCORES)))
        xs = [np.ascontiguousarray(res.results[c]["oT"]) for c in range(NCORES)]
    per = NCORES // BATCH
    out = np.empty((BATCH, SEQ, D), np.float32)
    for c in range(NCORES):
        b, s0 = c // per, (c % per) * TOK
        out[b, s0:s0 + TOK] = xs[c][:, HALO:].T
    return out
```

```python
import contextlib
import numpy as np
import concourse.bass as bass
import concourse.mybir as mybir
from concourse.bass_utils import run_bass_kernel_spmd

F32 = mybir.dt.float32
BF16 = mybir.dt.bfloat16
ALU = mybir.AluOpType
AF = mybir.ActivationFunctionType

D = 1024
BATCH = 2
SEQ = 8192
DEPTH = 2
NCORES = 8
TOK = 2048
HALO = 32
T = TOK + HALO
W = 416
NT = T // W
KC = D // 128
GW = 16 + W
EPS = 1e-6
POOL_WINDOWS = (2, 4, 8, 16)

PC_PRE, PC_POST, PC_CW, PC_CB, PC_PS = 0, 8, 16, 28, 32
PL = 36
PC_TAB = DEPTH * PL
PC_EPS = PC_TAB + 64
NPAR = PC_EPS + 1

FUSED = True


class Res:
    __slots__ = ("name", "last_w", "readers", "excl")

    def __init__(self, name, excl=False):
        self.name = name
        self.last_w = None
        self.readers = []
        self.excl = excl


class DmaSem:
    __slots__ = ("handle", "count")

    def __init__(self, handle):
        self.handle = handle
        self.count = 0


class Op:
    __slots__ = ("eng", "fn", "deps", "needs_inc", "dsem", "dval", "ticket")


ENGS = ("pe", "act", "dve", "pool", "sp")


class Prog:
    def __init__(self):
        self.q = {e: [] for e in ENGS}

    def add(self, eng, fn, reads=(), writes=(), dsem=None):
        op = Op()
        op.eng, op.fn, op.dsem, op.needs_inc, op.ticket, op.dval = eng, fn, dsem, False, None, None
        deps = []
        seen = set()

        def dep(o):
            if o is None or id(o) in seen:
                return
            seen.add(id(o))
            deps.append(o)

        for r in reads:
            dep(r.last_w)
            if r.excl:
                for o in r.readers:
                    if o.eng != eng:
                        dep(o)
        for w in writes:
            dep(w.last_w)
            for o in w.readers:
                dep(o)
        for r in reads:
            r.readers.append(op)
        for w in writes:
            w.last_w = op
            w.readers = []
        op.deps = [o for o in deps if not (eng == "pe" and o.eng == "pe" and o.dsem is None)]
        for o in op.deps:
            if o.dsem is None:
                o.needs_inc = True
        if dsem is not None:
            dsem.count += 1
            op.dval = 16 * dsem.count
        self.q[eng].append(op)
        return op

    def emit(self, eng, engine, esem):
        waited = {}
        for op in self.q[eng]:
            for o in op.deps:
                if o.dsem is not None:
                    key, h, val = id(o.dsem), o.dsem.handle, o.dval
                else:
                    key, h, val = o.eng, esem[o.eng], o.ticket
                if waited.get(key, 0) >= val:
                    continue
                engine.wait_ge(h, val)
                waited[key] = val
            if op.fn is None:
                continue
            inst = op.fn(engine)
            if op.dsem is not None:
                inst.then_inc(op.dsem.handle, 16)
            elif op.needs_inc:
                inst.then_inc(esem[eng], 1)

    def assign_tickets(self):
        for e in ENGS:
            n = 0
            for op in self.q[e]:
                if op.dsem is None and op.needs_inc:
                    n += 1
                    op.ticket = n


def build_nc(layers):
    nc = bass.Bass("TRN2", target_bir_lowering=False, dynamic_dma_scratch_size=8192)
    xT = nc.dram_tensor("xT", [D, T], F32, kind="ExternalInput").ap()
    w_in = nc.dram_tensor("w_in", [DEPTH, D, 3 * D], F32, kind="ExternalInput").ap()
    w_out = nc.dram_tensor("w_out", [DEPTH, D, D], F32, kind="ExternalInput").ap()
    w_pool = nc.dram_tensor("w_pool", [DEPTH, 4, 128, 128], F32, kind="ExternalInput").ap()
    par = nc.dram_tensor("par", [128, NPAR], F32, kind="ExternalInput").ap()
    oT = nc.dram_tensor("oT", [D, T], F32, kind="ExternalOutput").ap()

    xT_v = xT.rearrange("(k p) t -> p k t", p=128)
    oT_v = oT.rearrange("(k p) t -> p k t", p=128)

    P = Prog()
    with contextlib.ExitStack() as es:
        def sb(name, shape, dt):
            return es.enter_context(nc.sbuf_tensor(name, shape, dt))

        def sem(name):
            return es.enter_context(nc.semaphore(name))

        x_sb = sb("x_sb", [128, KC, T], F32)
        xs_sb = sb("xs_sb", [128, KC, T], BF16)
        y_sb = sb("y_sb", [128, KC, T], BF16)
        win_sb = [sb(f"win{s}", [128, KC, 512], BF16) for s in range(2)]
        wout_sb = sb("wout", [128, KC, D], BF16)
        wpool_sb = sb("wpool", [128, DEPTH, 4, 128], BF16)
        par_sb = sb("par_sb", [128, NPAR], F32)
        ones_sb = sb("ones", [128, 128], BF16)
        corr_sb = sb("corr", [128, 16], F32)
        utail = sb("utail", [128, 4, 16], F32)
        NG = 14
        G = [sb(f"g{i}", [128, GW], F32) for i in range(NG)]
        OB = [sb(f"ob{i}", [128, W], F32) for i in range(KC)]
        BL = [sb(f"bl{i}", [128, W], F32) for i in range(2)]
        BT = [sb(f"bt{i}", [128, W], F32) for i in range(2)]
        pbuf = [sb(f"pb{i}", [128, W], BF16) for i in range(2)]
        sqb = [sb(f"sq{i}", [128, W], BF16) for i in range(3)]
        sqp = [sb(f"sqp{i}", [128, W], BF16) for i in range(4)]
        banks = [es.enter_context(nc.psum_tensor(f"bank{i}", [128, 512], F32)) for i in range(8)]

        esem = {e: sem(f"e_{e}") for e in ENGS}
        ds_x = [DmaSem(sem(f"d_x{i}")) for i in range(NT)]
        ds_win = [DmaSem(sem(f"d_win{i}")) for i in range(2)]
        ds_wout = DmaSem(sem("d_wout"))
        ds_wp = [DmaSem(sem(f"d_wp{l}")) for l in range(DEPTH)]
        ds_par = DmaSem(sem("d_par"))
        ds_out = DmaSem(sem("d_out"))

        r_x = [[Res(f"x{i}_{k}") for k in range(KC)] for i in range(NT)]
        r_xs = [[Res(f"xs{i}_{k}") for k in range(KC)] for i in range(NT)]
        r_y = [[Res(f"y{i}_{k}") for k in range(KC)] for i in range(NT)]
        r_win = [Res(f"win{s}") for s in range(2)]
        r_wout = Res("wout")
        r_wpool = [Res(f"wpool{l}") for l in range(DEPTH)]
        r_par = Res("par")
        r_ones = Res("ones")
        r_corr = Res("corr")
        r_utail = [Res(f"utail{g}") for g in range(4)]
        r_G = [Res(f"g{i}") for i in range(NG)]
        r_Gh = [Res(f"gh{i}") for i in range(NG)]
        r_OB = [Res(f"ob{i}") for i in range(KC)]
        r_BL = [Res(f"bl{i}") for i in range(2)]
        r_BT = [Res(f"bt{i}") for i in range(2)]
        r_pb = [Res(f"pb{i}") for i in range(2)]
        r_sq = [Res(f"sq{i}") for i in range(3)]
        r_sqp = [Res(f"sqp{i}") for i in range(4)]
        r_bank = [Res(f"bank{i}", excl=True) for i in range(8)]

        state = {"bank": 0, "sq": 0, "g": {}, "pb": 0}

        def next_bank():
            b = state["bank"]
            state["bank"] = (b + 1) % 6
            return b

        def next_aux():
            b = state.get("aux", 0)
            state["aux"] = 1 - b
            return 6 + b

        def next_sq():
            b = state["sq"]
            state["sq"] = (b + 1) % 3
            return b

        def rot(name, choices):
            i = state["g"].get(name, 0)
            state["g"][name] = i + 1
            return choices[i % len(choices)]

        def tcols(i):
            return slice(i * W, (i + 1) * W)

        def pc(c):
            return par_sb[:, c:c + 1]

        P.add("sp", lambda e: e.dma_start(out=par_sb[:], in_=par[:, :]), writes=[r_par], dsem=ds_par)
        P.add("pool", lambda e: e.memset(ones_sb[:], 1.0), writes=[r_ones])
        ds_x0 = [DmaSem(sem(f"d_x0_{k}")) for k in range(KC)]
        for k in range(KC):
            P.add("sp", lambda e, k=k: e.dma_start(out=x_sb[:, k, tcols(0)], in_=xT_v[:, k, tcols(0)]),
                  writes=[r_x[0][k]], dsem=ds_x0[k])
        for i in range(1, NT):
            P.add("sp", lambda e, i=i: e.dma_start(out=x_sb[:, :, tcols(i)], in_=xT_v[:, :, tcols(i)]),
                  writes=r_x[i], dsem=ds_x[i])
        for l in layers:
            P.add("pool", lambda e, l=l: e.dma_start(out=wpool_sb[:, l, :, :],
                                                     in_=w_pool[l].rearrange("g c d -> c g d")),
                  writes=[r_wpool[l]], dsem=ds_wp[l])

        wstate = {"n": 0}

        def load_win(l, sc):
            slot = wstate["n"] % 2
            wstate["n"] += 1
            wv = w_in[l].rearrange("(k p) c -> p k c", p=128)
            off, ncol = (sc * 512, 512) if sc < 4 else (2048 + (sc - 4) * 256, 256)
            P.add("pool", lambda e: e.dma_start(out=win_sb[slot][:, :, 0:ncol], in_=wv[:, :, off:off + ncol]),
                  writes=[r_win[slot]], dsem=ds_win[slot])
            return slot

        def load_wout(l):
            wv = w_out[l].rearrange("(k p) c -> p k c", p=128)
            P.add("pool", lambda e: e.dma_start(out=wout_sb[:], in_=wv), writes=[r_wout], dsem=ds_wout)

        def rstd_from_bank(b, g_ln, g_ri):
            P.add("act", lambda e: e.activation(out=G[g_ln][:, 0:W], in_=banks[b][:, 0:W], func=AF.Ln,
                                                scale=1.0 / D, bias=pc(PC_EPS)),
                  reads=[r_bank[b], r_par], writes=[r_G[g_ln]])
            P.add("act", lambda e: e.activation(out=G[g_ri][:, 0:W], in_=G[g_ln][:, 0:W], func=AF.Exp, scale=-0.5),
                  reads=[r_G[g_ln]], writes=[r_G[g_ri]])
            return g_ri

        def prenorm_squares(i):
            b = 7
            for k in range(KC):
                s = next_sq()
                if k % 3 == 2:
                    P.add("pool", lambda e, s=s, k=k: e.tensor_tensor(out=sqb[s][:], in0=x_sb[:, k, tcols(i)],
                                                                      in1=x_sb[:, k, tcols(i)], op=ALU.mult),
                          reads=[r_x[i][k]], writes=[r_sq[s]])
                else:
                    P.add("act", lambda e, s=s, k=k: e.activation(out=sqb[s][:], in_=x_sb[:, k, tcols(i)],
                                                                  func=AF.Square),
                          reads=[r_x[i][k]], writes=[r_sq[s]])
                P.add("pe", lambda e, s=s, k=k: e.matmul(banks[b][:, 0:W], lhsT=ones_sb[:], rhs=sqb[s][:],
                                                         start=(k == 0), stop=(k == KC - 1)),
                      reads=[r_sq[s], r_ones], writes=[r_bank[b]])
            return b

        pn_buf = {}

        def prenorm_finish(l, i, b, part=None):
            if part in (None, 0):
                pn_buf[(l, i)] = rot("pn", [12, 13])
                rstd_from_bank(b, pn_buf[(l, i)], pn_buf[(l, i)])
            g_ri = pn_buf[(l, i)]
            for k in (range(KC) if part is None else range(4 * part, 4 * part + 4)):
                P.add("dve", lambda e, k=k: e.scalar_tensor_tensor(
                    out=xs_sb[:, k, tcols(i)], in0=x_sb[:, k, tcols(i)], scalar=pc(l * PL + PC_PRE + k),
                    op0=ALU.mult, in1=G[g_ri][:, 0:W], op1=ALU.mult),
                    reads=[r_x[i][k], r_G[g_ri], r_par], writes=[r_xs[i][k]])

        def flush(pe_list):
            for fn, rd, wr in pe_list:
                P.add("pe", fn, reads=rd, writes=wr)
            pe_list.clear()

        def prenorm_tile(l, i):
            b = prenorm_squares(i)
            prenorm_finish(l, i, b)

        def prenorm_gen(l, i):
            b = 7

            def squares(k0):
                for k in range(k0, k0 + 4):
                    if k % 4 == 1:
                        P.add("pool", lambda e, k=k: e.tensor_tensor(out=sqp[k % 4][:], in0=x_sb[:, k, tcols(i)],
                                                                     in1=x_sb[:, k, tcols(i)], op=ALU.mult),
                              reads=[r_x[i][k]], writes=[r_sqp[k % 4]])
                    else:
                        P.add("act", lambda e, k=k: e.activation(out=sqp[k % 4][:], in_=x_sb[:, k, tcols(i)],
                                                                 func=AF.Square),
                              reads=[r_x[i][k]], writes=[r_sqp[k % 4]])

            def ones(k0):
                for k in range(k0, k0 + 4):
                    P.add("pe", lambda e, k=k: e.matmul(banks[b][:, 0:W], lhsT=ones_sb[:], rhs=sqp[k % 4][:],
                                                        start=(k == 0), stop=(k == KC - 1)),
                          reads=[r_sqp[k % 4], r_ones], writes=[r_bank[b]])
            squares(0)
            yield
            yield
            ones(0)
            squares(4)
            yield
            ones(4)
            yield
            prenorm_finish(l, i, b, 0)
            yield
            prenorm_finish(l, i, b, 1)
            yield

        def proj_group(slot, q, i):
            b = next_bank()

            def fn(e):
                ins = None
                for k in range(KC):
                    ins = e.matmul(banks[b][:, 0:W], lhsT=win_sb[slot][:, k, q * 128:(q + 1) * 128],
                                   rhs=xs_sb[:, k, tcols(i)], start=(k == 0), stop=(k == KC - 1))
                return ins
            P.add("pe", fn, reads=r_xs[i] + [r_win[slot]], writes=[r_bank[b]])
            return b

        def step_A(l, j, i, slot, prev, ret):
            base = l * PL
            g_c = rot("c", [0, 1])
            g_ch = rot("ch", [2, 3])
            g_acc = rot("acc", [4, 5])
            g_sz = rot("sz", [6, 7])
            ret[0] = g_ch
            bc = proj_group(slot, 1, i)
            P.add("act", lambda e: e.activation(out=G[g_c][:, 0:W], in_=banks[bc][:, 0:W], func=AF.Copy),
                  reads=[r_bank[bc]], writes=[r_G[g_c]])
            yield
            bh = proj_group(slot, 2, i)
            P.add("dve", lambda e: e.tensor_tensor(out=G[g_ch][:, 16:GW], in0=G[g_c][:, 0:W], in1=banks[bh][:, 0:W],
                                                   op=ALU.mult),
                  reads=[r_G[g_c], r_bank[bh]], writes=[r_G[g_ch]])
            if prev is None:
                P.add("pool", lambda e: e.memset(G[g_ch][:, 0:16], 0.0), writes=[r_Gh[g_ch]])
            else:
                P.add("pool", lambda e: e.tensor_copy(out=G[g_ch][:, 0:16], in_=G[prev][:, W:GW]),
                      reads=[r_G[prev]], writes=[r_Gh[g_ch]])
            yield
            bz = proj_group(slot, 3, i)
            P.add("act", lambda e: e.activation(out=G[g_sz][:, 0:W], in_=banks[bz][:, 0:W], func=AF.Silu),
                  reads=[r_bank[bz]], writes=[r_G[g_sz]])
            P.add("pool", lambda e: e.tensor_scalar(out=G[g_acc][:, 0:W], in0=G[g_ch][:, 16:GW],
                                                    scalar1=pc(base + PC_CW + 2 * 4 + j), scalar2=pc(base + PC_CB + j),
                                                    op0=ALU.mult, op1=ALU.add),
                  reads=[r_G[g_ch], r_par], writes=[r_G[g_acc]])
            P.add("dve", lambda e: e.scalar_tensor_tensor(
                out=G[g_acc][:, 0:W], in0=G[g_ch][:, 15:GW - 1], scalar=pc(base + PC_CW + 1 * 4 + j), op0=ALU.mult,
                in1=G[g_acc][:, 0:W], op1=ALU.add),
                reads=[r_G[g_ch], r_Gh[g_ch], r_G[g_acc], r_par], writes=[r_G[g_acc]])
            P.add("dve", lambda e: e.scalar_tensor_tensor(
                out=G[g_acc][:, 0:W], in0=G[g_ch][:, 14:GW - 2], scalar=pc(base + PC_CW + 0 * 4 + j), op0=ALU.mult,
                in1=G[g_acc][:, 0:W], op1=ALU.add),
                reads=[r_G[g_ch], r_Gh[g_ch], r_G[g_acc], r_par], writes=[r_G[g_acc]])
            yield
            bb = proj_group(slot, 0, i)
            P.add("dve", lambda e: e.tensor_tensor(out=G[g_acc][:, 0:W], in0=G[g_acc][:, 0:W], in1=banks[bb][:, 0:W],
                                                   op=ALU.mult),
                  reads=[r_G[g_acc], r_bank[bb]], writes=[r_G[g_acc]])
            P.add("pool", lambda e: e.tensor_tensor(out=y_sb[:, j, tcols(i)], in0=G[g_acc][:, 0:W], in1=G[g_sz][:, 0:W],
                                                    op=ALU.mult),
                  reads=[r_G[g_acc], r_G[g_sz]], writes=[r_y[i][j]])
            yield

        def step_B(l, g, i, slot, prev, deferred):
            base = l * PL
            w = POOL_WINDOWS[g]
            bu = proj_group(slot, 0, i)
            bz = proj_group(slot, 1, i)
            g_u = rot("u", [8, 9])
            g_sz = rot("sz", [6, 7])
            sA, sB = 10, 11
            pb = state["pb"]
            state["pb"] = 1 - pb
            P.add("act", lambda e: e.activation(out=G[g_u][:, 16:GW], in_=banks[bu][:, 0:W], func=AF.Copy),
                  reads=[r_bank[bu]], writes=[r_G[g_u]])
            if i == 0:
                P.add("pool", lambda e: e.memset(G[g_u][:, 0:16], 0.0), writes=[r_Gh[g_u]])
            else:
                P.add("pool", lambda e: e.tensor_copy(out=G[g_u][:, 0:16], in_=utail[:, g, :]),
                      reads=[r_utail[g]], writes=[r_Gh[g_u]])
            if i < NT - 1:
                P.add("pool", lambda e: e.tensor_copy(out=utail[:, g, :], in_=G[g_u][:, W:GW]),
                      reads=[r_G[g_u]], writes=[r_utail[g]])
            P.add("act", lambda e: e.activation(out=G[g_sz][:, 0:W], in_=banks[bz][:, 0:W], func=AF.Silu),
                  reads=[r_bank[bz]], writes=[r_G[g_sz]])
            src, dst = g_u, sA
            v = 2
            while v <= w:
                st = 16 - (w - v)
                h = v // 2
                rd = [r_G[src], r_Gh[src]]
                wr = [r_G[dst], r_Gh[dst]]
                P.add("dve", lambda e, src=src, dst=dst, st=st, h=h: e.tensor_tensor(
                    out=G[dst][:, st:GW], in0=G[src][:, st:GW], in1=G[src][:, st - h:GW - h], op=ALU.add),
                    reads=rd, writes=wr)
                src = dst
                dst = sB if dst == sA else sA
                v *= 2
            s_fin = src
            P.add("dve", lambda e: e.scalar_tensor_tensor(
                out=pbuf[pb][:, 0:W], in0=G[s_fin][:, 16:GW], scalar=1.0 / w, op0=ALU.mult,
                in1=G[g_u][:, 16:GW], op1=ALU.subtract),
                reads=[r_G[s_fin], r_G[g_u]], writes=[r_pb[pb]])
            if i == 0:
                c0 = HALO
                P.add("dve", lambda e: e.tensor_tensor(out=corr_sb[:, 0:16], in0=G[s_fin][:, 16 + c0:32 + c0],
                                                       in1=par_sb[:, PC_TAB + 16 * g:PC_TAB + 16 * g + 16], op=ALU.mult),
                      reads=[r_G[s_fin], r_par], writes=[r_corr])
                P.add("dve", lambda e: e.tensor_tensor(out=pbuf[pb][:, c0:c0 + 16], in0=corr_sb[:, 0:16],
                                                       in1=G[g_u][:, 16 + c0:32 + c0], op=ALU.subtract),
                      reads=[r_corr, r_G[g_u], r_pb[pb]], writes=[r_pb[pb]])

            def later():
                bp = next_bank()
                P.add("pe", lambda e: e.matmul(banks[bp][:, 0:W], lhsT=wpool_sb[:, l, g, :], rhs=pbuf[pb][:, 0:W],
                                               start=True, stop=True),
                      reads=[r_pb[pb], r_wpool[l]], writes=[r_bank[bp]])
                P.add("dve", lambda e: e.scalar_tensor_tensor(
                    out=y_sb[:, 4 + g, tcols(i)], in0=banks[bp][:, 0:W], scalar=pc(base + PC_PS + g), op0=ALU.mult,
                    in1=G[g_sz][:, 0:W], op1=ALU.mult),
                    reads=[r_bank[bp], r_G[g_sz], r_par], writes=[r_y[i][4 + g]])
            deferred.append(later)
            return g_u

        def phase_B_tile(l, i, last):
            base = l * PL
            bs = 6
            pend = []
            if last and i == NT - 1:
                P.add("act", lambda e: e.activation(out=corr_sb[:, 0:2], in_=corr_sb[:, 2:4], func=AF.Exp),
                      reads=[r_corr], writes=[r_corr])
            for m in range(KC):
                b = next_bank()

                def fn(e, b=b, m=m):
                    ins = None
                    for k in range(KC):
                        ins = e.matmul(banks[b][:, 0:W], lhsT=wout_sb[:, k, m * 128:(m + 1) * 128],
                                       rhs=y_sb[:, k, tcols(i)], start=(k == 0), stop=(k == KC - 1))
                    return ins
                P.add("pe", fn, reads=r_y[i] + [r_wout], writes=[r_bank[b]])
                flush(pend)
                s = next_sq()
                P.add("act", lambda e, b=b, s=s: e.activation(out=sqb[s][:], in_=banks[b][:, 0:W], func=AF.Square),
                      reads=[r_bank[b]], writes=[r_sq[s]])
                P.add("act", lambda e, b=b, m=m: e.activation(out=OB[m][:], in_=banks[b][:, 0:W], func=AF.Copy),
                      reads=[r_bank[b]], writes=[r_OB[m]])
                pend.append((lambda e, s=s, m=m: e.matmul(banks[bs][:, 0:W], lhsT=ones_sb[:], rhs=sqb[s][:],
                                                          start=(m == 0), stop=(m == KC - 1)),
                             [r_sq[s], r_ones], [r_bank[bs]]))
                if m in (2, 5):
                    yield
            yield
            flush(pend)
            yield
            bl = rot("bl", [0, 1])
            P.add("act", lambda e: e.activation(out=BL[bl][:], in_=banks[bs][:, 0:W], func=AF.Ln,
                                                scale=1.0 / D, bias=pc(PC_EPS)),
                  reads=[r_bank[bs], r_par], writes=[r_BL[bl]])
            P.add("act", lambda e: e.activation(out=BL[bl][:], in_=BL[bl][:], func=AF.Exp, scale=-0.5),
                  reads=[r_BL[bl]], writes=[r_BL[bl]])
            for m in range(KC):
                t = m % 2
                P.add("dve", lambda e, m=m, t=t: e.scalar_tensor_tensor(
                    out=BT[t][:], in0=OB[m][:], scalar=pc(base + PC_POST + m), op0=ALU.mult,
                    in1=BL[bl][:], op1=ALU.mult),
                    reads=[r_OB[m], r_BL[bl], r_par], writes=[r_BT[t]])
                add_eng = "pool" if (last and i == NT - 1 and m % 2 == 0) else "dve"
                P.add(add_eng, lambda e, m=m, t=t: e.tensor_tensor(
                    out=x_sb[:, m, tcols(i)], in0=x_sb[:, m, tcols(i)], in1=BT[t][:], op=ALU.add),
                    reads=[r_x[i][m], r_BT[t]], writes=[r_x[i][m]])
                if last:
                    P.add("sp", lambda e, m=m: e.dma_start(out=oT_v[:, m, tcols(i)], in_=x_sb[:, m, tcols(i)]),
                          reads=[r_x[i][m]], dsem=ds_out)
                if m in (2, 5, 7):
                    yield

        SC_ORDER = [4, 5, 6, 7, 0, 1, 2, 3]
        nL = len(layers)
        blocks = [(li, pos) for li in range(nL) for pos in range(8)]

        def interleave(gens, cycles=None):
            gens = list(gens)
            n = 0
            while gens and (cycles is None or n < cycles):
                for gen in gens[:]:
                    try:
                        next(gen)
                    except StopIteration:
                        gens.remove(gen)
                n += 1
            return gens

        def one_shot(fn):
            fn()
            yield

        pn_at = {}
        for li in range(1, nL):
            for t in range(NT):
                key = (li - 1, 7, t + 3) if t + 3 < NT else (li, 0, t + 3 - NT)
                pn_at.setdefault(key, []).append((li, t))

        order = []
        for bi, (li, pos) in enumerate(blocks):
            if li > 0 and pos == 0:
                order += [(bi, 0), (bi, 1), (bi, 2), (bi + 1, 0), (bi + 1, 1), (bi + 1, 2),
                          (bi, 3), (bi, 4), (bi + 1, 3), (bi + 1, 4)]
            elif li > 0 and pos == 1:
                continue
            else:
                order += [(bi, i) for i in range(NT)]

        def load_blk(bi):
            bli, bpos = blocks[bi]
            return load_win(layers[bli], SC_ORDER[bpos])

        slots = {0: load_blk(0)}
        if len(blocks) > 1:
            slots[1] = load_blk(1)
        deferred = []
        carry = []
        prevs = {}
        done = {}
        for (bi, i) in order:
            li, pos = blocks[bi]
            l = layers[li]
            sc = SC_ORDER[pos]
            if pos == 1 and i == 0:
                load_wout(l)
            slot = slots[bi]
            prev = prevs.get(bi)
            if li == 0 and pos == 0:
                if i == 0:
                    prenorm_tile(l, 0)
                    for t in range(1, NT):
                        prenorm_tile(l, t)
            pn_gens = [prenorm_gen(layers[pli], t) for (pli, t) in pn_at.get((li, pos, i), [])]
            run = deferred[:]
            deferred.clear()
            gens = []
            ret = [None]
            if sc < 4:
                gens.append(step_A(l, sc, i, slot, prev, ret))
            else:
                def do_B(sc=sc, i=i, slot=slot, prev=prev, ret=ret, l=l):
                    ret[0] = step_B(l, sc - 4, i, slot, prev, deferred)
                gens.append(one_shot(do_B))
            if pos == 7 and i >= 1:
                gens.append(phase_B_tile(l, i - 1, li == nL - 1))
            if pos == 0 and li > 0 and i == 0:
                gens.append(phase_B_tile(layers[li - 1], NT - 1, False))
            carry = interleave(carry + gens + pn_gens, cycles=4)
            prevs[bi] = ret[0]
            for fn in run:
                fn()
            done[bi] = done.get(bi, 0) + 1
            if done[bi] == NT and bi + 2 < len(blocks):
                slots[bi + 2] = load_blk(bi + 2)
        for fn in deferred:
            fn()
        deferred.clear()
        interleave(carry + [phase_B_tile(layers[nL - 1], NT - 1, True)])

        P.add("sp", None, reads=[], writes=[])
        P.q["sp"][-1].deps = [o for o in P.q["sp"] if o.dsem is ds_out][-1:]

        P.assign_tickets()
        with nc.Block() as block:
            @block.tensor
            def _(e):
                P.emit("pe", e, esem)

            @block.scalar
            def _(e):
                P.emit("act", e, esem)

            @block.vector
            def _(e):
                P.emit("dve", e, esem)

            @block.gpsimd
            def _(e):
                P.emit("pool", e, esem)

            @block.sync
            def _(e):
                P.emit("sp", e, esem)
    return nc


def _pack_params(pre_norm, conv_w, conv_b, pool_scale, post_norm, core):
    par = np.zeros((128, NPAR), np.float32)
    for l in range(DEPTH):
        b = l * PL
        par[:, b + PC_PRE:b + PC_PRE + 8] = pre_norm[l].reshape(8, 128).T
        par[:, b + PC_POST:b + PC_POST + 8] = post_norm[l].reshape(8, 128).T
        for tap in range(3):
            par[:, b + PC_CW + tap * 4:b + PC_CW + tap * 4 + 4] = conv_w[l, tap].reshape(4, 128).T
        par[:, b + PC_CB:b + PC_CB + 4] = conv_b[l].reshape(4, 128).T
        par[:, b + PC_PS:b + PC_PS + 4] = pool_scale[l].reshape(4, 128).T
    seq_start = (core % (NCORES // BATCH) == 0)
    for g, w in enumerate(POOL_WINDOWS):
        for j in range(16):
            cnt = min(j + 1, w) if seq_start else w
            par[:, PC_TAB + 16 * g + j] = np.float32(1.0) / np.float32(cnt)
    par[:, PC_EPS] = EPS
    return par


def _prep_w_in(w_in):
    w_in = np.asarray(w_in, np.float32)
    cols = []
    for j in range(4):
        for q in range(4):
            cols.append(np.arange(j * 128 + q * 512, j * 128 + q * 512 + 128))
    for g in range(4):
        cols.append(np.arange(2048 + g * 128, 2048 + g * 128 + 128))
        cols.append(np.arange(2560 + g * 128, 2560 + g * 128 + 128))
    return np.ascontiguousarray(w_in[:, :, np.concatenate(cols)])


_NC_CACHE = {}


def _get_nc(layers):
    key = tuple(layers)
    if key not in _NC_CACHE:
        _NC_CACHE[key] = build_nc(list(layers))
    return _NC_CACHE[key]


def _shard_x(x):
    per = NCORES // BATCH
    out = []
    for c in range(NCORES):
        b, s0 = c // per, (c % per) * TOK
        reg = np.zeros((T, D), np.float32)
        lo = s0 - HALO
        if lo < 0:
            reg[HALO:] = x[b, 0:TOK]
        else:
            reg[:] = x[b, lo:s0 + TOK]
        out.append(np.ascontiguousarray(reg.T))
    return out


def kernel(x, pre_norm, w_in, conv_w, conv_b, w_pool, pool_scale, w_out, post_norm):
    x = np.asarray(x, np.float32)
    w_in = _prep_w_in(w_in)
    w_out = np.ascontiguousarray(np.asarray(w_out, np.float32))
    w_pool = np.ascontiguousarray(np.asarray(w_pool, np.float32))
    pre_norm, post_norm = np.asarray(pre_norm, np.float32), np.asarray(post_norm, np.float32)
    conv_w, conv_b = np.asarray(conv_w, np.float32), np.asarray(conv_b, np.float32)
    pool_scale = np.asarray(pool_scale, np.float32)
    pars = [_pack_params(pre_norm, conv_w, conv_b, pool_scale, post_norm, c) for c in range(NCORES)]
    xs = _shard_x(x)
    launches = [list(range(DEPTH))] if FUSED else [[l] for l in range(DEPTH)]
    for layers in launches:
        nc = _get_nc(layers)
        in_maps = [{"xT": xs[c], "w_in": w_in, "w_out": w_out, "w_pool": w_pool, "par": pars[c]}
                   for c in range(NCORES)]
        res = run_bass_kernel_spmd(nc, in_maps, core_ids=list(range(NCORES)))
        xs = [np.ascontiguousarray(res.results[c]["oT"]) for c in range(NCORES)]
    per = NCORES // BATCH
    out = np.empty((BATCH, SEQ, D), np.float32)
    for c in range(NCORES):
        b, s0 = c // per, (c % per) * TOK
        out[b, s0:s0 + TOK] = xs[c][:, HALO:].T
    return out
```
